# Optimizing a Trainium2 kernel written in Bass

```python
import math
import jax, jax.numpy as jnp
from jax import lax
import numpy as np

D_MODEL = 1024
BATCH = 4
SEQ = 4096
DEPTH = 1

N_META = 16
CONV_WIDTH = D_MODEL
CONV_KERNEL = 31
POOL_WIDTH = D_MODEL
POOL_WINDOWS = (2, 4, 8, 16)
N_POOL_GROUPS = len(POOL_WINDOWS)
POOL_GROUP_DIM = POOL_WIDTH // N_POOL_GROUPS
N_BRANCHES = 2
D_FF = int(math.ceil(8 * D_MODEL / 3 / 256) * 256)
D_IN = 2 * CONV_WIDTH + POOL_WIDTH + N_BRANCHES * D_MODEL
RMS_EPS = 1e-6
LN_EPS = 1e-5

kernel_name = "hybrid_conv_pool_gated_encoder_block"


def rms_norm(x, g):
    xf = x.astype(jnp.float32)
    y = xf * lax.rsqrt(jnp.mean(xf * xf, axis=-1, keepdims=True) + RMS_EPS)
    return (y * g.astype(jnp.float32)).astype(x.dtype)


def layer_norm(x, g, b):
    xf = x.astype(jnp.float32)
    mu = jnp.mean(xf, axis=-1, keepdims=True)
    xc = xf - mu
    var = jnp.mean(xc * xc, axis=-1, keepdims=True)
    y = xc * lax.rsqrt(var + LN_EPS)
    return (y * g.astype(jnp.float32) + b.astype(jnp.float32)).astype(x.dtype)


def conformer_conv(a_val, a_gate, w_dw, b_dw, ln_g, ln_b, w_conv_out):
    a = a_val * jax.nn.sigmoid(a_gate)
    pad = CONV_KERNEL // 2
    a = lax.conv_general_dilated(
        a, w_dw[:, None, :].astype(a.dtype),
        window_strides=(1,), padding=[(pad, pad)],
        dimension_numbers=("NWC", "WIO", "NWC"),
        feature_group_count=a.shape[-1]) + b_dw
    a = jax.nn.silu(layer_norm(a, ln_g, ln_b))
    return a @ w_conv_out


def multi_scale_pool(p, w_pool, pool_scale, w_pool_out):
    B, T, C = p.shape
    pf = p.astype(jnp.float32)
    cs = jnp.concatenate([jnp.zeros((B, 1, C), jnp.float32), jnp.cumsum(pf, axis=1)], axis=1)
    t = jnp.arange(T)
    outs = []
    for k, w in enumerate(POOL_WINDOWS):
        left = w // 2
        right = w - 1 - left
        lo = jnp.clip(t - left, 0, T)
        hi = jnp.clip(t + right + 1, 0, T)
        sl = slice(k * POOL_GROUP_DIM, (k + 1) * POOL_GROUP_DIM)
        csg = cs[..., sl]
        s = jnp.take(csg, hi, axis=1) - jnp.take(csg, lo, axis=1)
        cnt = (hi - lo).astype(jnp.float32)[None, :, None]
        outs.append(s / cnt - pf[..., sl])
    m = jnp.stack(outs, axis=2).astype(p.dtype)
    m = jnp.einsum("btgc,gcd->btgd", m, w_pool).reshape(B, T, C)
    return (m * pool_scale) @ w_pool_out


def setup_inputs(seed: int = 0) -> dict:
    key = jax.random.key(seed)
    ks = jax.random.split(key, 24)
    f32 = jnp.float32
    L, D = DEPTH, D_MODEL

    def nrm(k, shape, fan_in):
        return jax.random.normal(k, shape, f32) * (fan_in ** -0.5)

    def gain(k, shape):
        return 1.0 + 0.05 * jax.random.normal(k, shape, f32)

    return {
        "x": jax.random.normal(ks[0], (BATCH, SEQ, D), f32),
        "meta_tokens": jax.random.normal(ks[1], (N_META, D), f32),
        "g_mix": gain(ks[2], (L, D)),
        "w_in": nrm(ks[3], (L, D, D_IN), D),
        "b_gate": 0.02 * jax.random.normal(ks[4], (L, N_BRANCHES * D), f32),
        "w_dw": nrm(ks[5], (L, CONV_KERNEL, CONV_WIDTH), CONV_KERNEL),
        "b_dw": 0.02 * jax.random.normal(ks[6], (L, CONV_WIDTH), f32),
        "ln_g": gain(ks[7], (L, CONV_WIDTH)),
        "ln_b": 0.02 * jax.random.normal(ks[8], (L, CONV_WIDTH), f32),
        "w_conv_out": nrm(ks[9], (L, CONV_WIDTH, D), CONV_WIDTH),
        "w_pool": nrm(ks[10], (L, N_POOL_GROUPS, POOL_GROUP_DIM, POOL_GROUP_DIM), POOL_GROUP_DIM),
        "pool_scale": gain(ks[11], (L, POOL_WIDTH)),
        "w_pool_out": nrm(ks[12], (L, POOL_WIDTH, D), POOL_WIDTH),
        "w_o": nrm(ks[13], (L, D, D), D),
        "g_ffn": gain(ks[14], (L, D)),
        "w_ffn_gate": nrm(ks[15], (L, D, D_FF), D),
        "w_ffn_up": nrm(ks[16], (L, D, D_FF), D),
        "w_ffn_down": nrm(ks[17], (L, D_FF, D), D_FF),
        "g_final": gain(ks[18], (D,)),
    }


def reference(x, meta_tokens, g_mix, w_in, b_gate, w_dw, b_dw, ln_g, ln_b, w_conv_out,
              w_pool, pool_scale, w_pool_out, w_o, g_ffn, w_ffn_gate, w_ffn_up,
              w_ffn_down, g_final):
    B = x.shape[0]
    meta = jnp.broadcast_to(meta_tokens[None].astype(x.dtype), (B, N_META, D_MODEL))
    h = jnp.concatenate([meta, x], axis=1)
    c1 = CONV_WIDTH
    c2 = 2 * CONV_WIDTH
    c3 = c2 + POOL_WIDTH
    c4 = c3 + D_MODEL
    for l in range(DEPTH):
        u = rms_norm(h, g_mix[l])
        z = u @ w_in[l]
        gates = jax.nn.sigmoid(z[..., c3:] + b_gate[l])
        y_conv = conformer_conv(z[..., :c1], z[..., c1:c2], w_dw[l], b_dw[l],
                                ln_g[l], ln_b[l], w_conv_out[l])
        y_pool = multi_scale_pool(z[..., c2:c3], w_pool[l], pool_scale[l], w_pool_out[l])
        merged = gates[..., :D_MODEL] * y_conv + gates[..., D_MODEL:] * y_pool
        h = h + merged @ w_o[l]
        v = rms_norm(h, g_ffn[l])
        f = jax.nn.silu(v @ w_ffn_gate[l]) * (v @ w_ffn_up[l])
        h = h + f @ w_ffn_down[l]
    h = rms_norm(h, g_final)
    return h[:, N_META:, :]
```

```python
import numpy as np
import ml_dtypes
import concourse.bass as bass
import concourse.mybir as mybir
from concourse.bass_utils import run_bass_kernel_spmd

F32 = mybir.dt.float32
BF16 = mybir.dt.bfloat16
ALU = mybir.AluOpType
AF = mybir.ActivationFunctionType

D = 1024
KC = 8
SEQ = 4096
BATCH = 4
NMETA = 16
DFF = 2816
NFC = DFF // 128
NCORES = 8
TOK = 2048
TT = 512
NT = TOK // TT
HALO = 15
EXT = TT + 2 * HALO
EXTP = 544
XROWS = TOK + 2 * HALO
KW = 31
WINS = (2, 4, 8, 16)
RMS_EPS = 1e-6
LN_EPS = 1e-5
NSLOT = 6
RW = 544
CHUNK_EL = 4096
GRAN = 64
PSUM_G0 = 10_000_000
DRAM_G0 = 20_000_000

PC_BGA, PC_BGB, PC_BDW, PC_LNG, PC_LNB, PC_PSC, PC_WDW = 0, 8, 16, 24, 32, 40, 48
PC_COLS = 48 + KC * 4 * 8


class Prod:
    def __init__(self, sem, inc):
        self.sem = sem
        self.inc = inc
        self.count = 0


class Rec(Prod):
    def __init__(self, name, sem, is_pe=False):
        super().__init__(sem, 1)
        self.name = name
        self.is_pe = is_pe
        self.ops = []
        self.waited = {}


class Sched:
    def __init__(self, nc):
        self.nc = nc
        self.bases = {}
        self.lw = {}
        self.rd = {}
        self.gcache = {}

    def gran(self, ap):
        name = ap.tensor.name
        if name not in self.bases:
            return ()
        key = (name, ap.offset, tuple(ap.ap), str(ap.dtype))
        g = self.gcache.get(key)
        if g is not None:
            return g
        base = self.bases[name]
        esz = mybir.dt.size(ap.dtype)
        dims = [(int(s), int(c)) for (s, c) in list(ap.ap)[1:]]
        if not dims:
            dims = [(1, 1)]
        ls, lc = dims[-1]
        outer = dims[:-1]
        span = ((lc - 1) * abs(ls) + 1) * esz if ls != 0 else esz
        offs = [int(ap.offset) * esz]
        for (s, c) in outer:
            if s == 0 or c == 1:
                continue
            offs = [o + i * s * esz for o in offs for i in range(c)]
        gs = set()
        for o in offs:
            lo = (base + o)
            hi = lo + span
            gs.update(range(lo // GRAN, (hi - 1) // GRAN + 1))
        g = frozenset(gs)
        self.gcache[key] = g
        return g

    def _deps(self, rg, wg, prod, val):
        deps = {}

        def upd(p, v):
            if deps.get(p, 0) < v:
                deps[p] = v
        lw, rd = self.lw, self.rd
        for g in rg:
            w = lw.get(g)
            if w is not None:
                upd(*w)
        for g in wg:
            w = lw.get(g)
            if w is not None:
                upd(*w)
            r = rd.get(g)
            if r:
                for p, v in r.items():
                    upd(p, v)
        for g in rg:
            r = rd.get(g)
            if r is None:
                rd[g] = {prod: val}
            else:
                r[prod] = val
        for g in wg:
            lw[g] = (prod, val)
            rd[g] = None
        return deps

    def _waits(self, rec, deps):
        for p, v in deps.items():
            if p is rec:
                if rec.is_pe:
                    continue
                assert v <= rec.count, "self-dependency on unsignaled op"
            if rec.waited.get(p, 0) >= v:
                continue
            rec.waited[p] = v
            rec.ops.append(("wait", p.sem, v))

    def _sets(self, reads, writes, xr, xw):
        rg = set(xr)
        for a in reads:
            rg.update(self.gran(a))
        wg = set(xw)
        for a in writes:
            wg.update(self.gran(a))
        return rg, wg

    def op(self, rec, fn, reads, writes, signal=True, xr=(), xw=()):
        rg, wg = self._sets(reads, writes, xr, xw)
        deps = self._deps(rg, wg, rec, rec.count + 1)
        self._waits(rec, deps)
        rec.ops.append(("ins", fn, signal))
        if signal:
            rec.count += 1

    def dma(self, rec, dsem, out_ap, in_ap, xr=(), xw=(), val=None):
        rg, wg = self._sets([in_ap], [out_ap], xr, xw)
        if val is None:
            val = dsem.count + 16
        deps = self._deps(rg, wg, dsem, val)
        self._waits(rec, deps)
        rec.ops.append(("dma", out_ap, in_ap, dsem.sem))
        dsem.count += 16

    def wait_prod(self, rec, prod):
        if prod.count > 0 and rec.waited.get(prod, 0) < prod.count:
            rec.waited[prod] = prod.count
            rec.ops.append(("wait", prod.sem, prod.count))


def replay(e, rec):
    for o in rec.ops:
        if o[0] == "wait":
            e.wait_ge(o[1], o[2])
        elif o[0] == "ins":
            r = o[1](e)
            if o[2]:
                r.then_inc(rec.sem, 1)
        else:
            e.dma_start(out=o[1], in_=o[2]).then_inc(o[3], 16)


def build_nc(NT=NT):
    TOK = NT * TT
    XROWS = TOK + 2 * HALO
    nc = bass.Bass("TRN2", target_bir_lowering=False)
    S = Sched(nc)

    xe = nc.dram_tensor("xe", [XROWS, D], F32, kind="ExternalInput")
    w_in = nc.dram_tensor("w_in", [D, 5 * D], F32, kind="ExternalInput")
    w_co = nc.dram_tensor("w_conv_out", [D, D], F32, kind="ExternalInput")
    w_pl = nc.dram_tensor("w_pool", [4, 256, 256], F32, kind="ExternalInput")
    w_po = nc.dram_tensor("w_pool_out", [D, D], F32, kind="ExternalInput")
    w_o = nc.dram_tensor("w_o", [D, D], F32, kind="ExternalInput")
    w_fg = nc.dram_tensor("w_ffn_gate", [D, DFF], F32, kind="ExternalInput")
    w_fu = nc.dram_tensor("w_ffn_up", [D, DFF], F32, kind="ExternalInput")
    w_fd = nc.dram_tensor("w_ffn_down", [DFF, D], F32, kind="ExternalInput")
    pc_d = nc.dram_tensor("pc", [128, PC_COLS], F32, kind="ExternalInput")
    gb_d = nc.dram_tensor("gb", [128, 3 * D], F32, kind="ExternalInput")
    id_d = nc.dram_tensor("ident", [128, 128], BF16, kind="ExternalInput")
    ic_d = nc.dram_tensor("invc", [128, 32], F32, kind="ExternalInput")
    em_d = nc.dram_tensor("emask", [128, 32], BF16, kind="ExternalInput")
    out_d = nc.dram_tensor("out", [TOK, D], F32, kind="ExternalOutput")

    chunks = []

    def rect(w, kc0, nkc, c0, ncols):
        src = w.ap().rearrange("(kc p) c -> p kc c", p=128)[:, kc0:kc0 + nkc, c0:c0 + ncols]

        def dst(sc, nkc=nkc, ncols=ncols):
            return sc[:, 0:nkc * ncols].rearrange("p (k c) -> p k c", k=nkc)
        return (dst, src), nkc * ncols

    def add(name, w, kc0, nkc, c0, ncols):
        pr, nel = rect(w, kc0, nkc, c0, ncols)
        chunks.append((name, [pr], nel))

    add("val0", w_in, 0, 8, 0 * D, 512)
    add("gate0", w_in, 0, 8, 1 * D, 512)
    add("val1", w_in, 0, 8, 0 * D + 512, 512)
    add("gate1", w_in, 0, 8, 1 * D + 512, 512)
    chunks.append(("cl0", None, CHUNK_EL))
    add("pool0", w_in, 0, 8, 2 * D, 512)
    chunks.append(("cl1", None, CHUNK_EL))
    add("pool1", w_in, 0, 8, 2 * D + 512, 512)
    prs = []
    for g in range(4):
        src = w_pl.ap()[g].rearrange("(k p) d -> p k d", p=128)

        def dst(sc, g=g):
            return sc[:, g * 512:(g + 1) * 512].rearrange("p (k d) -> p k d", k=2)
        prs.append((dst, src))
    chunks.append(("wpool", prs, 2048))
    for h in range(2):
        add(f"gA{h}", w_in, 0, 8, 3 * D + h * 512, 512)
        add(f"gB{h}", w_in, 0, 8, 4 * D + h * 512, 512)
        add(f"po{h}", w_po, 0, 8, h * 512, 512)
        add(f"co{h}", w_co, 0, 8, h * 512, 512)
    for h in range(2):
        add(f"wo{h}", w_o, 0, 8, h * 512, 512)
    FB = [(b * 512, 512) for b in range(5)] + [(2560, 256)]
    for b, (f0, fw) in enumerate(FB):
        add(f"fg{b}", w_fg, 0, 8, f0, fw)
        add(f"fu{b}", w_fu, 0, 8, f0, fw)
    DG = [(0, 8), (8, 8), (16, 6)]
    for h in range(2):
        for gi, (j0, nj) in enumerate(DG):
            add(f"fd{h}_{gi}", w_fd, j0, nj, h * 512, 512)
    NCH = len(chunks)
    cidx = {c[0]: i for i, c in enumerate(chunks)}
    wsc = nc.dram_tensor("wsc", [NCH, 128, CHUNK_EL], BF16, kind="Internal")
    asc = nc.dram_tensor("asc", [2, 32, 4, 2, EXTP], BF16, kind="Internal")

    base0 = (nc.sbuf_base + 63) // 64 * 64
    top = nc.sbuf_top
    cur = [base0]

    def alloc_at(name, shape, dt, off):
        t = nc.alloc_sbuf_tensor_at(name, list(shape), dt, offset=off)
        S.bases[t.name] = off
        return t

    def alloc(name, shape, dt):
        sz = int(np.prod(shape[1:])) * mybir.dt.size(dt)
        off = cur[0]
        cur[0] += (sz + 63) // 64 * 64
        return alloc_at(name, shape, dt, off)

    wslot = [alloc(f"wslot{i}", [128, CHUNK_EL], BF16) for i in range(NSLOT)]
    uT = [alloc(f"uT{i}", [128, KC, EXTP], BF16) for i in range(2)]
    gb = alloc("gb_s", [128, 3, D], F32)
    pc = alloc("pc_s", [128, PC_COLS], F32)
    ident = alloc("ident_s", [128, 128], BF16)
    ones = alloc("ones_s", [128, 128], F32)
    nhalf = alloc("nhalf_s", [128, 16], F32)
    invc = alloc("invc_s", [128, 32], F32)
    emask = alloc("emask_s", [128, 32], BF16)
    repS = [alloc(f"rep{i}", [128, 4, 2, RW], BF16) for i in range(2)]
    sqj = alloc("sqj", [128, D], BF16)
    ub = [alloc(f"ub{i}", [128, D], BF16) for i in range(2)]
    NST = 16
    stats = alloc("stats", [128, NST, 16], F32)
    acarry = alloc("acarry", [128, KC, 32], BF16)
    pcarry = alloc("pcarry", [128, KC, 32], F32)
    tmp8 = alloc("tmp8", [128, 16], F32)

    regA = cur[0]
    aT = alloc("aT", [128, KC, EXTP], BF16)
    sg = [alloc(f"sg{i}", [128, EXT], F32) for i in range(2)]
    psb = [alloc(f"psb{i}", [128, EXT], F32) for i in range(2)]
    s1 = alloc("s1", [128, EXT], F32)
    s2 = alloc("s2", [128, EXT], F32)
    mT = alloc("mT", [128, KC, TT], BF16)
    d_sb = alloc("d_sb", [128, KC, TT], F32)
    dsq = [alloc(f"dsq{i}", [128, TT], F32) for i in range(2)]
    tn = [alloc(f"tn{i}", [128, TT], F32) for i in range(2)]
    tn2 = [alloc(f"tn2{i}", [128, TT], F32) for i in range(2)]
    dst_off = cur[0]
    actT = alloc("actT", [128, KC, TT], BF16)
    m2T = alloc("m2T", [128, KC, TT], BF16)
    lstage = alloc_at("lstage", [128, 128, 32], BF16, dst_off)
    mean = alloc("mean", [128, TT], F32)
    msq = alloc("msq", [128, TT], F32)
    rstd_ln = alloc("rstd_ln", [128, TT], F32)
    mergedT = alloc("mergedT", [128, KC, TT], BF16)
    sgA = [alloc(f"sgA{i}", [128, TT], F32) for i in range(2)]
    sgB = [alloc(f"sgB{i}", [128, TT], F32) for i in range(2)]
    t1 = [alloc("t1_0", [128, TT], F32)] * 2
    t2 = [alloc(f"t2_{i}", [128, TT], F32) for i in range(2)]
    endA = cur[0]
    assert endA <= top, (endA, top)
    cur[0] = regA
    h1 = [alloc(f"h1_{i}", [128, D], F32) for i in range(4)]
    xres = [alloc(f"xres{i}", [128, D], F32) for i in range(4)]
    xt = [xres[0], xres[1]]
    vT = alloc("vT", [128, KC, TT], BF16)
    fT = alloc("fT", [128, NFC, TT], BF16)
    sgf = [alloc(f"sgf{i}", [128, TT], F32) for i in range(2)]
    outb = [alloc(f"outb{i}", [128, D], F32) for i in range(2)]
    assert cur[0] <= endA, (cur[0], endA)

    psum = []
    for i in range(8):
        t = nc.alloc_psum_tensor(f"ps{i}", [128, 512], F32)
        S.bases[t.name] = PSUM_G0 * GRAN + i * 2048
        psum.append(t)
    psum_bf = [p.bitcast(BF16) for p in psum]
    ring = [0]

    def nb():
        b = ring[0]
        ring[0] = (b + 1) % 6
        return b

    def sem(name):
        return nc.alloc_semaphore(name)

    PE = Rec("pe", sem("s_pe"), is_pe=True)
    ACT = Rec("act", sem("s_act"))
    DVE = Rec("dve", sem("s_dve"))
    POOL = Rec("pool", sem("s_pool"))
    SP = Rec("sp", sem("s_sp"))
    cst = Prod(sem("d_const"), 16)
    slot_sem = [Prod(sem(f"d_slot{i}"), 16) for i in range(NSLOT)]
    slot_sem_sw = [Prod(sem(f"d_slotsw{i}"), 16) for i in range(NSLOT)]
    cast_sem = [Prod(sem(f"d_cast{i}"), 16) for i in range(NCH)]
    xt_sem = [Prod(sem(f"d_xt{i}"), 16) for i in range(2)]
    xres_sem = [Prod(sem(f"d_xres{i}"), 16) for i in range(4)]
    out_sem = [Prod(sem(f"d_out{i}"), 16) for i in range(2)]
    rep_sem = [Prod(sem(f"d_rep{i}"), 16) for i in range(2)]
    a2d_sem = [Prod(sem(f"d_a2d{i}"), 16) for i in range(2)]

    const_dmas = [
        (gb[:].rearrange("p a d -> p (a d)"), gb_d.ap()),
        (pc[:], pc_d.ap()),
        (ident[:], id_d.ap()),
        (invc[:], ic_d.ap()),
        (emask[:], em_d.ap()),
    ]
    for (o, i_) in const_dmas:
        S.dma(SP, cst, o, i_, val=16 * len(const_dmas))
    S.op(POOL, lambda e: e.memset(ones[:], 1.0), [], [ones[:]])
    S.op(POOL, lambda e: e.memset(nhalf[:], -0.5), [], [nhalf[:]])
    for r_ in repS:
        S.op(POOL, lambda e, r_=r_: e.memset(r_[:], 0.0), [], [r_[:]])

    def pcc(col, c):
        return pc[:, col + c:col + c + 1]

    def cast(n):
        name, prs, nel = chunks[n]
        if prs is None:
            h = int(name[2:])
            do = lstage[:]
            i0 = emask[:].unsqueeze(1).to_broadcast([128, 128, 32])
            wsl = pc[:, PC_WDW + h * 128:PC_WDW + (h + 1) * 128]
            i1 = wsl.unsqueeze(2).to_broadcast([128, 128, 32])
            S.op(POOL, lambda e, do=do, i0=i0, i1=i1: e.tensor_tensor(out=do, in0=i0, in1=i1, op=ALU.mult),
                 [emask[:], wsl], [do])
            S.dma(POOL, cast_sem[n], wsc.ap()[n][:, 0:nel], do.rearrange("p k c -> p (k c)"),
                  xw=[DRAM_G0 + n])
            return
        for pi, (dst, src) in enumerate(prs):
            S.dma(POOL, cast_sem[n], dst(wsc.ap()[n]), src, xw=([DRAM_G0 + n] if pi == 0 else ()),
                  val=16 * len(prs))

    seq = [n for _ in range(NT) for n in range(NCH)]
    NLOADS = len(seq)
    wstate = {"next": 0, "released": 0, "li": 0, "cast": 0}
    CAST_AHEAD = 8

    def load_next():
        li = wstate["next"]
        assert li < wstate["released"] + NSLOT
        n = seq[li]
        s = li % NSLOT
        nel = chunks[n][2]
        prs = chunks[n][1]
        if li < NCH:
            while wstate["cast"] < min(NCH, li + CAST_AHEAD):
                cast(wstate["cast"])
                wstate["cast"] += 1
        S.dma(SP, slot_sem[s], wslot[s][:, 0:nel], wsc.ap()[n][:, 0:nel], xr=[DRAM_G0 + n])
        wstate["next"] = li + 1

    def prefetch():
        while wstate["next"] < min(NLOADS, wstate["released"] + NSLOT):
            load_next()

    def wchunk(name):
        li = wstate["li"]
        assert chunks[seq[li]][0] == name, (chunks[seq[li]][0], name)
        while wstate["next"] <= li:
            load_next()
        wstate["li"] = li + 1
        return wslot[li % NSLOT]

    def release(k):
        if wstate.get("defer"):
            wstate["pending"] = wstate.get("pending", 0) + k
            return
        wstate["released"] += k
        prefetch()

    def defer_releases(on):
        wstate["defer"] = on
        if not on and wstate.get("pending", 0):
            k = wstate["pending"]
            wstate["pending"] = 0
            release(k)

    stat_i = [0]

    def stat_slot():
        i = stat_i[0]
        stat_i[0] = (i + 1) % NST
        return stats[:, i, :]

    def mm(out, lhsT, rhs, start, stop, signal):
        S.op(PE, lambda e: e.matmul(out, lhsT=lhsT, rhs=rhs, start=start, stop=stop),
             [lhsT, rhs], [out], signal=signal)

    def rms_rstd(src, n, gi_unused=None):
        st = stat_slot()
        ssq, ms, rs = st[0:n, 0:1], st[0:n, 1:2], st[0:n, 2:3]
        S.op(ACT, lambda e: e.activation(out=sqj[0:n, :], in_=src, func=AF.Square, accum_out=ssq),
             [src], [sqj[0:n, :], ssq])
        S.op(DVE, lambda e: e.tensor_scalar(out=ms, in0=ssq, scalar1=1.0 / D, scalar2=RMS_EPS,
                                            op0=ALU.mult, op1=ALU.add), [ssq], [ms])
        S.op(POOL, lambda e: e.tensor_tensor(out=rs, in0=ms, in1=nhalf[0:n, 0:1], op=ALU.pow),
             [ms, nhalf[0:n, 0:1]], [rs])
        return rs

    def norm_chain(src, n, gidx, q):
        rs = rms_rstd(src, n)
        u = ub[q][0:n, :]
        g_ap = gb[0:n, gidx, :]
        S.op(DVE, lambda e: e.scalar_tensor_tensor(out=u, in0=src, scalar=rs, in1=g_ap,
                                                   op0=ALU.mult, op1=ALU.mult), [src, rs, g_ap], [u])

    def norm_transpose(n, dstT, col0, q):
        b = nb()
        pb = psum_bf[b]
        for k in range(KC):
            o = pb[:, k * 128:k * 128 + n]
            i_ = ub[q][0:n, k * 128:(k + 1) * 128]
            idn = ident[0:n, 0:n]
            S.op(PE, lambda e, o=o, i_=i_, idn=idn: e.transpose(out=o, in_=i_, identity=idn),
                 [i_, idn], [o], signal=(k == KC - 1))
        src3 = pb[:].rearrange("p (k c) -> p k c", k=KC)[:, :, 0:n]
        dst3 = dstT[:, :, col0:col0 + n]
        S.op(ACT, lambda e: e.activation(out=dst3, in_=src3, func=AF.Copy), [src3], [dst3])

    def stage0_subtiles(i):
        if i == 0:
            return [(r, min(128, EXT - r)) for r in range(0, EXT, 128)]
        return [(2 * HALO + r, 128) for r in range(0, TT, 128)]

    def s0_chain(i, idx):
        r, n = stage0_subtiles(i)[idx]
        q = idx % 2
        row0 = i * TT + r
        S.dma(ACT, xt_sem[q], xt[q][0:n, :], xe.ap()[row0:row0 + n, :])
        norm_chain(xt[q][0:n, :], n, 0, q)

    def s0_transpose(i, idx):
        r, n = stage0_subtiles(i)[idx]
        norm_transpose(n, uT[i % 2], r, idx % 2)

    def stage0_carry(i):
        if i > 0:
            src = uT[(i - 1) % 2][:, :, TT:EXT]
            dst = uT[i % 2][:, :, 0:2 * HALO]
            S.op(DVE, lambda e: e.tensor_copy(out=dst, in_=src), [src], [dst])

    def tile_body(i):
        u = uT[i % 2]
        last = (i == NT - 1)
        pieces = [(0, TT), (TT, 2 * HALO)] if i == 0 else [(2 * HALO, TT)]

        if i > 0:
            dsta = aT[:, :, 0:2 * HALO]
            S.op(POOL, lambda e: e.tensor_copy(out=dsta, in_=acarry[:, :, 0:2 * HALO]), [acarry[:, :, 0:2 * HALO]], [dsta])
        wts = {}
        pend = []

        def stats_mm(c, j):
            mm(psum[6][:], ones[:], d_sb[:, c, :], c == 0, c == KC - 1, c == KC - 1)
            mm(psum[7][:], ones[:], dsq[j][:], c == 0, c == KC - 1, c == KC - 1)

        def vg(c):
            h, cc = c // 4, c % 4
            if cc == 0:
                wts["v"] = wchunk(f"val{h}")
                wts["g"] = wchunk(f"gate{h}")
            wv, wg = wts["v"], wts["g"]
            for (c0, n) in pieces:
                bv, bg = nb(), nb()
                for k in range(KC):
                    mm(psum[bv][:, 0:n], wv[:, k * 512 + cc * 128:k * 512 + cc * 128 + 128],
                       u[:, k, c0:c0 + n], k == 0, k == KC - 1, k == KC - 1)
                for k in range(KC):
                    mm(psum[bg][:, 0:n], wg[:, k * 512 + cc * 128:k * 512 + cc * 128 + 128],
                       u[:, k, c0:c0 + n], k == 0, k == KC - 1, k == KC - 1)
                j = c % 2
                sgo = sg[j][:, 0:n]
                pgi = psum[bg][:, 0:n]
                pvi = psum[bv][:, 0:n]
                ao = aT[:, c, c0:c0 + n]
                S.op(ACT, lambda e, sgo=sgo, pgi=pgi: e.activation(out=sgo, in_=pgi, func=AF.Sigmoid),
                     [pgi], [sgo])
                S.op(DVE, lambda e, ao=ao, pvi=pvi, sgo=sgo: e.tensor_tensor(out=ao, in0=pvi, in1=sgo, op=ALU.mult),
                     [pvi, sgo], [ao])
            if cc == 3:
                release(2)

        def im2col(pr):
            st = pr % 2
            tok = DRAM_G0 + 5000 + st
            for j in range(4):
                S.dma(SP, a2d_sem[st], asc.ap()[st][:, j, :, 0:EXT], aT[32 * j:32 * j + 32, 2 * pr:2 * pr + 2, 0:EXT],
                      xw=([tok] if j == 0 else ()), val=a2d_sem[st].count + 16 * (4 - j))
            srcv = asc.ap()[st].rearrange("c j e x -> c (j e) x")
            tot = rep_sem[st].count + 16 * 4
            for s_ in range(4):
                ws = 540 if s_ < 3 else 539
                o = repS[st][32 * s_:32 * s_ + 32, :, :, 0:ws].rearrange("p j e x -> p (j e) x")
                S.dma(SP, rep_sem[st], o, srcv[:, :, s_:s_ + ws], xr=[tok], val=tot)

        def conv(c):
            h, cc = c // 4, c % 4
            if cc == 0:
                wts["l"] = wchunk(f"cl{h}")
            wl = wts["l"]
            st, e_ = (c // 2) % 2, c % 2
            bc = nb()
            for q in range(8):
                for j in range(4):
                    col = ((cc * 4 + j) * 8 + q) * 32
                    o = psum[bc][32 * j:32 * j + 32, :]
                    lt = wl[:, col:col + 32]
                    r_ = repS[st][:, j, e_, 4 * q:4 * q + TT]
                    lastmm = (q == 7 and j == 3)
                    S.op(PE, lambda e, o=o, lt=lt, r_=r_, q=q, j=j: e.matmul(
                        o, lhsT=lt, rhs=r_, start=(q == 0), stop=(q == 7), tile_position=(0, 32 * j)),
                        [lt, r_], [o], signal=lastmm)
            if c == 3:
                release(1)
            dso = d_sb[:, c, :]
            pci = psum[bc][:]
            bia = pcc(PC_BDW, c)
            S.op(ACT, lambda e, dso=dso, pci=pci, bia=bia: e.activation(out=dso, in_=pci, func=AF.Identity, bias=bia, scale=1.0),
                 [pci, bia], [dso])
            j2 = c % 2
            dqo = dsq[j2][:]
            S.op(ACT, lambda e, dqo=dqo, dso=dso: e.activation(out=dqo, in_=dso, func=AF.Square), [dso], [dqo])
            pend.append((c, j2))
            if len(pend) > 1:
                stats_mm(*pend.pop(0))

        def poolcols(h):
            wp = wchunk(f"pool{h}")
            PENG = POOL if h == 0 else DVE
            for cc in range(4):
                c = h * 4 + cc
                j = c % 2
                p = psb[j]
                if i > 0:
                    dstp = p[:, 0:2 * HALO]
                    srcp = pcarry[:, c, 0:2 * HALO]
                    S.op(PENG, lambda e, dstp=dstp, srcp=srcp: e.tensor_copy(out=dstp, in_=srcp), [srcp], [dstp])
                for (c0, n) in pieces:
                    bp = nb()
                    for k in range(KC):
                        mm(psum[bp][:, 0:n], wp[:, k * 512 + cc * 128:k * 512 + cc * 128 + 128],
                           u[:, k, c0:c0 + n], k == 0, k == KC - 1, k == KC - 1)
                    po = p[:, c0:c0 + n]
                    pi_ = psum[bp][:, 0:n]
                    S.op(ACT, lambda e, po=po, pi_=pi_: e.activation(out=po, in_=pi_, func=AF.Copy), [pi_], [po])
                if not last:
                    srcp = p[:, TT:EXT]
                    dstp = pcarry[:, c, 0:2 * HALO]
                    S.op(PENG, lambda e, dstp=dstp, srcp=srcp: e.tensor_copy(out=dstp, in_=srcp), [srcp], [dstp])
                g = c // 2
                w = WINS[g]
                srcb, width = p, EXT
                step = 1
                tgl = 0
                while step < w:
                    dstb = (s1, s2)[tgl]
                    nw = width - step
                    a0 = srcb[:, 0:nw]
                    a1 = srcb[:, step:step + nw]
                    do = dstb[:, 0:nw]
                    S.op(PENG, lambda e, do=do, a0=a0, a1=a1: e.tensor_tensor(out=do, in0=a0, in1=a1, op=ALU.add),
                         [a0, a1], [do])
                    srcb, width = dstb, nw
                    step *= 2
                    tgl ^= 1
                off = HALO - w // 2
                Sw = srcb[:, off:off + TT]
                pc_ = p[:, HALO:HALO + TT]
                mo = mT[:, c, :]
                S.op(DVE, lambda e, mo=mo, Sw=Sw, pc_=pc_, w=w: e.scalar_tensor_tensor(
                    out=mo, in0=Sw, scalar=1.0 / w, in1=pc_, op0=ALU.mult, op1=ALU.subtract), [Sw, pc_], [mo])
                if last:
                    Sw8 = srcb[:, off + TT - 8:off + TT]
                    ic8 = invc[:, g * 8:(g + 1) * 8]
                    p8 = p[:, HALO + TT - 8:HALO + TT]
                    t8 = tmp8[:, 0:8]
                    mo8 = mT[:, c, TT - 8:TT]
                    S.op(DVE, lambda e, t8=t8, Sw8=Sw8, ic8=ic8: e.tensor_tensor(out=t8, in0=Sw8, in1=ic8, op=ALU.mult),
                         [Sw8, ic8], [t8])
                    S.op(DVE, lambda e, mo8=mo8, t8=t8, p8=p8: e.tensor_tensor(out=mo8, in0=t8, in1=p8, op=ALU.subtract),
                         [t8, p8], [mo8])

        vg(0); vg(1); im2col(0)
        defer_releases(True)
        vg(2); vg(3); im2col(1)
        vg(4); vg(5)
        conv(0); conv(1); im2col(2)
        defer_releases(False)
        defer_releases(True)
        vg(6); vg(7)
        conv(2); conv(3); im2col(3)
        defer_releases(False)
        poolcols(0)
        release(1)
        conv(4); conv(5)
        poolcols(1)
        conv(6); conv(7)
        release(2)
        if not last:
            srca = aT[:, :, TT:EXT]
            S.op(POOL, lambda e, srca=srca: e.tensor_copy(out=acarry[:, :, 0:2 * HALO], in_=srca), [srca], [acarry[:, :, 0:2 * HALO]])
        stats_mm(*pend.pop(0))
        wpl = wchunk("wpool")

        def wpool_group(gi):
            g, jo = gi // 2, gi % 2
            c = 2 * g + jo
            bw = nb()
            for k in range(2):
                mm(psum[bw][:], wpl[:, g * 512 + k * 256 + jo * 128:g * 512 + k * 256 + jo * 128 + 128],
                   mT[:, 2 * g + k, :], k == 0, k == 1, k == 1)
            m2o = m2T[:, c, :]
            pwi = psum[bw][:]
            sc = pcc(PC_PSC, c)
            S.op(ACT, lambda e, m2o=m2o, pwi=pwi, sc=sc: e.activation(out=m2o, in_=pwi, func=AF.Identity, scale=sc),
                 [pwi, sc], [m2o])
            if gi == 7:
                release(1)

        def ln_stats():
            S.op(DVE, lambda e: e.tensor_scalar(out=mean[:], in0=psum[6][:], scalar1=1.0 / D, scalar2=None, op0=ALU.mult),
                 [psum[6][:]], [mean[:]])
            S.op(DVE, lambda e: e.tensor_tensor(out=msq[:], in0=mean[:], in1=mean[:], op=ALU.mult), [mean[:]], [msq[:]])
            S.op(DVE, lambda e: e.scalar_tensor_tensor(out=msq[:], in0=psum[7][:], scalar=1.0 / D, in1=msq[:],
                                                       op0=ALU.mult, op1=ALU.subtract), [psum[7][:], msq[:]], [msq[:]])
            S.op(DVE, lambda e: e.tensor_scalar(out=msq[:], in0=msq[:], scalar1=LN_EPS, scalar2=None, op0=ALU.add),
                 [msq[:]], [msq[:]])

        def ln_sqrt():
            S.op(ACT, lambda e: e.activation(out=rstd_ln[:], in_=msq[:], func=AF.Sqrt), [msq[:]], [rstd_ln[:]])

        def ln_recip():
            S.op(DVE, lambda e: e.reciprocal(out=rstd_ln[:], in_=rstd_ln[:]), [rstd_ln[:]], [rstd_ln[:]])

        def ln_norm(c):
            j = c % 2
            dso = d_sb[:, c, :]
            tno = tn[j][:]
            tn2o = tn2[j][:]
            ao = actT[:, c, :]
            lg, lb = pcc(PC_LNG, c), pcc(PC_LNB, c)
            S.op(DVE, lambda e, tno=tno, dso=dso: e.tensor_tensor(out=tno, in0=dso, in1=mean[:], op=ALU.subtract),
                 [dso, mean[:]], [tno])
            S.op(DVE, lambda e, tn2o=tn2o, tno=tno: e.tensor_tensor(out=tn2o, in0=tno, in1=rstd_ln[:], op=ALU.mult),
                 [tno, rstd_ln[:]], [tn2o])
            S.op(ACT, lambda e, ao=ao, tn2o=tn2o, lg=lg, lb=lb: e.activation(out=ao, in_=tn2o, func=AF.Silu, bias=lb, scale=lg),
                 [tn2o, lg, lb], [ao])

        mw = {}

        def wsl(wt, k, cc):
            return wt[:, k * 512 + cc * 128:k * 512 + cc * 128 + 128]

        gbanks = {}

        def GA(c):
            h, cc = c // 4, c % 4
            j = c % 2
            if cc == 0:
                mw["ga"] = wchunk(f"gA{h}")
                mw["gb"] = wchunk(f"gB{h}")
            bA, bB = nb(), nb()
            for k in range(KC):
                mm(psum[bA][:], wsl(mw["ga"], k, cc), u[:, k, HALO:HALO + TT], k == 0, k == KC - 1, k == KC - 1)
            for k in range(KC):
                mm(psum[bB][:], wsl(mw["gb"], k, cc), u[:, k, HALO:HALO + TT], k == 0, k == KC - 1, k == KC - 1)
            sa, sb_ = sgA[j][:], sgB[j][:]
            pa, pb_ = psum[bA][:], psum[bB][:]
            ba, bb = pcc(PC_BGA, c), pcc(PC_BGB, c)
            S.op(ACT, lambda e, sa=sa, pa=pa, ba=ba: e.activation(out=sa, in_=pa, func=AF.Sigmoid, bias=ba, scale=1.0),
                 [pa, ba], [sa])
            S.op(ACT, lambda e, sb_=sb_, pb_=pb_, bb=bb: e.activation(out=sb_, in_=pb_, func=AF.Sigmoid, bias=bb, scale=1.0),
                 [pb_, bb], [sb_])

        def GP(c):
            h, cc = c // 4, c % 4
            j = c % 2
            if cc == 0:
                mw["po"] = wchunk(f"po{h}")
            bP = nb()
            for k in range(KC):
                mm(psum[bP][:], wsl(mw["po"], k, cc), m2T[:, k, :], k == 0, k == KC - 1, k == KC - 1)
            if cc == 3:
                release(3)
            sb_ = sgB[j][:]
            pp = psum[bP][:]
            t2o = t2[j][:]
            S.op(DVE, lambda e, t2o=t2o, pp=pp, sb_=sb_: e.tensor_tensor(out=t2o, in0=pp, in1=sb_, op=ALU.mult),
                 [pp, sb_], [t2o])

        def Y(c):
            h, cc = c // 4, c % 4
            j = c % 2
            if cc == 0:
                mw["co"] = wchunk(f"co{h}")
            bC = nb()
            for k in range(KC):
                mm(psum[bC][:], wsl(mw["co"], k, cc), actT[:, k, :], k == 0, k == KC - 1, k == KC - 1)
            if cc == 3:
                release(1)
            sa = sgA[j][:]
            pcv = psum[bC][:]
            t1o, t2o = t1[0][:], t2[j][:]
            S.op(DVE, lambda e, t1o=t1o, pcv=pcv, sa=sa: e.tensor_tensor(out=t1o, in0=pcv, in1=sa, op=ALU.mult),
                 [pcv, sa], [t1o])
            mo = mergedT[:, c, :]
            S.op(POOL, lambda e, mo=mo, t1o=t1o, t2o=t2o: e.tensor_tensor(out=mo, in0=t1o, in1=t2o, op=ALU.add),
                 [t1o, t2o], [mo])

        ln_stats()
        GA(0)
        ln_sqrt()
        ln_recip()
        GA(1)
        for gi in range(5):
            wpool_group(gi)
        ln_norm(0)
        for gi in range(5, 8):
            wpool_group(gi)
        ln_norm(1)
        ln_norm(2)
        GP(0)
        ln_norm(3)
        GP(1)
        for c in range(4, KC):
            ln_norm(c)
        for c in range(KC):
            Y(c)
            if c + 2 < KC:
                GA(c + 2)
                GP(c + 2)

        for s in range(4):
            row0 = i * TT + HALO + s * 128
            S.dma(ACT, xres_sem[s], xres[s][:], xe.ap()[row0:row0 + 128, :])
        wos = [wchunk("wo0"), wchunk("wo1")]
        for s_ in range(4):
            for h in range(2):
                bo = nb()
                for k in range(KC):
                    mm(psum[bo][:], mergedT[:, k, s_ * 128:(s_ + 1) * 128], wos[h][:, k * 512:(k + 1) * 512],
                       k == 0, k == KC - 1, k == KC - 1)
                ho = h1[s_][:, h * 512:(h + 1) * 512]
                xi = xres[s_][:, h * 512:(h + 1) * 512]
                poi = psum[bo][:]
                S.op(DVE, lambda e, ho=ho, poi=poi, xi=xi: e.tensor_tensor(out=ho, in0=poi, in1=xi, op=ALU.add),
                     [poi, xi], [ho])
            norm_chain(h1[s_][:], 128, 1, s_ % 2)
            if s_ >= 1:
                norm_transpose(128, vT, (s_ - 1) * 128, (s_ - 1) % 2)
        release(2)
        nxt = stage0_subtiles(i + 1) if i + 1 < NT else []
        ffw = {0: (wchunk("fg0"), wchunk("fu0"))}
        NSPL = 2
        pre = {}
        H2 = TT // 2
        for jj in range(NSPL):
            bG, bU = nb(), nb()
            pre[jj] = (bG, bU)
            for wt_, bk in ((ffw[0][0], bG), (ffw[0][1], bU)):
                for k in range(KC):
                    mm(psum[bk][:, 0:H2], wt_[:, k * 512 + jj * 128:k * 512 + jj * 128 + 128], vT[:, k, 0:H2],
                       k == 0, k == KC - 1, k == KC - 1)
        norm_transpose(128, vT, 3 * 128, 1)
        if nxt:
            stage0_carry(i + 1)
            s0_chain(i + 1, 0)

        for b, (f0, fw) in enumerate(FB):
            if b == 0:
                wg_, wu_ = ffw[0]
            else:
                wg_ = wchunk(f"fg{b}")
                wu_ = wchunk(f"fu{b}")
            for jj in range(fw // 128):
                jf = f0 // 128 + jj
                if b == 0 and jj < NSPL:
                    bG, bU = pre[jj]
                    for wt_, bk in ((wg_, bG), (wu_, bU)):
                        for k in range(KC):
                            mm(psum[bk][:, H2:TT], wt_[:, k * fw + jj * 128:k * fw + jj * 128 + 128], vT[:, k, H2:TT],
                               k == 0, k == KC - 1, k == KC - 1)
                else:
                    bG, bU = nb(), nb()
                    for k in range(KC):
                        mm(psum[bG][:], wg_[:, k * fw + jj * 128:k * fw + jj * 128 + 128], vT[:, k, :],
                           k == 0, k == KC - 1, k == KC - 1)
                    for k in range(KC):
                        mm(psum[bU][:], wu_[:, k * fw + jj * 128:k * fw + jj * 128 + 128], vT[:, k, :],
                           k == 0, k == KC - 1, k == KC - 1)
                jx = jf % 2
                so = sgf[jx][:]
                pgi, pui = psum[bG][:], psum[bU][:]
                fo = fT[:, jf, :]
                S.op(ACT, lambda e, so=so, pgi=pgi: e.activation(out=so, in_=pgi, func=AF.Silu), [pgi], [so])
                S.op(DVE, lambda e, fo=fo, pui=pui, so=so: e.tensor_tensor(out=fo, in0=pui, in1=so, op=ALU.mult),
                     [pui, so], [fo])
            release(2)
            if b + 1 < len(nxt):
                s0_chain(i + 1, b + 1)
            if b < len(nxt):
                s0_transpose(i + 1, b)
        oq = [0]
        for h in range(2):
            wd = [wchunk(f"fd{h}_{gi}") for gi in range(3)]
            for s in range(4):
                bd = nb()
                for jf in range(NFC):
                    mm(psum[bd][:], fT[:, jf, s * 128:(s + 1) * 128],
                       wd[jf // 8][:, (jf % 8) * 512:(jf % 8) * 512 + 512], jf == 0, jf == NFC - 1, jf == NFC - 1)
                ho = h1[s][:, h * 512:(h + 1) * 512]
                pdi = psum[bd][:]
                S.op(DVE, lambda e, ho=ho, pdi=pdi: e.tensor_tensor(out=ho, in0=pdi, in1=ho, op=ALU.add),
                     [pdi, ho], [ho])
                if h == 1:
                    rs = rms_rstd(h1[s][:], 128)
                    q = oq[0]
                    oq[0] ^= 1
                    oo = outb[q][:]
                    hs = h1[s][:]
                    gf = gb[:, 2, :]
                    S.op(DVE, lambda e, oo=oo, hs=hs, rs=rs, gf=gf: e.scalar_tensor_tensor(
                        out=oo, in0=hs, scalar=rs, in1=gf, op0=ALU.mult, op1=ALU.mult), [hs, rs, gf], [oo])
                    r0 = i * TT + s * 128
                    S.dma(POOL, out_sem[q], out_d.ap()[r0:r0 + 128, :], oo)
            release(3)

    for n in range(6):
        cast(n)
    wstate["cast"] = 6
    nsub0 = len(stage0_subtiles(0))
    s0_chain(0, 0)
    for idx in range(nsub0):
        if idx + 1 < nsub0:
            s0_chain(0, idx + 1)
        s0_transpose(0, idx)
    prefetch()
    last_dg = max(n for n in range(NCH) if chunks[n][1] is None)
    while wstate["cast"] <= last_dg:
        cast(wstate["cast"])
        wstate["cast"] += 1
    for i in range(NT):
        tile_body(i)
    for q in range(2):
        S.wait_prod(POOL, out_sem[q])

    with nc.Block() as block:
        @block.tensor
        def _(e):
            replay(e, PE)

        @block.scalar
        def _(e):
            replay(e, ACT)

        @block.vector
        def _(e):
            replay(e, DVE)

        @block.gpsimd
        def _(e):
            replay(e, POOL)

        @block.sync
        def _(e):
            replay(e, SP)
    return nc


def _host_inputs(inputs):
    f32 = np.float32
    x = np.asarray(inputs["x"], f32)
    meta = np.asarray(inputs["meta_tokens"], f32)

    def fm(v):
        return np.ascontiguousarray(np.asarray(v, f32).reshape(KC, 128).T)

    pc = np.zeros((128, PC_COLS), f32)
    bg = np.asarray(inputs["b_gate"], f32)[0]
    pc[:, PC_BGA:PC_BGA + 8] = fm(bg[:D])
    pc[:, PC_BGB:PC_BGB + 8] = fm(bg[D:])
    pc[:, PC_BDW:PC_BDW + 8] = fm(inputs["b_dw"][0])
    pc[:, PC_LNG:PC_LNG + 8] = fm(inputs["ln_g"][0])
    pc[:, PC_LNB:PC_LNB + 8] = fm(inputs["ln_b"][0])
    pc[:, PC_PSC:PC_PSC + 8] = fm(inputs["pool_scale"][0])
    wdw = np.asarray(inputs["w_dw"], f32)[0]
    wpad = np.concatenate([wdw, np.zeros((1, D), f32)], 0)
    w6 = wpad.reshape(8, 4, KC, 4, 32)
    wrep = w6.transpose(1, 4, 2, 3, 0).reshape(128, KC * 4 * 8)
    pc[:, PC_WDW:] = wrep
    emask = np.zeros((128, 32), ml_dtypes.bfloat16)
    for s_ in range(4):
        emask[32 * s_ + np.arange(32), np.arange(32)] = 1
    gb = np.stack([np.asarray(inputs["g_mix"], f32)[0], np.asarray(inputs["g_ffn"], f32)[0],
                   np.asarray(inputs["g_final"], f32)], 0).reshape(1, 3 * D)
    gb = np.ascontiguousarray(np.broadcast_to(gb, (128, 3 * D)))
    ident = np.eye(128, dtype=ml_dtypes.bfloat16)
    T = SEQ + NMETA
    shared = {
        "w_in": np.ascontiguousarray(np.asarray(inputs["w_in"], f32)[0]),
        "w_conv_out": np.ascontiguousarray(np.asarray(inputs["w_conv_out"], f32)[0]),
        "w_pool": np.ascontiguousarray(np.asarray(inputs["w_pool"], f32)[0]),
        "w_pool_out": np.ascontiguousarray(np.asarray(inputs["w_pool_out"], f32)[0]),
        "w_o": np.ascontiguousarray(np.asarray(inputs["w_o"], f32)[0]),
        "w_ffn_gate": np.ascontiguousarray(np.asarray(inputs["w_ffn_gate"], f32)[0]),
        "w_ffn_up": np.ascontiguousarray(np.asarray(inputs["w_ffn_up"], f32)[0]),
        "w_ffn_down": np.ascontiguousarray(np.asarray(inputs["w_ffn_down"], f32)[0]),
        "pc": pc, "gb": gb, "ident": ident, "emask": emask,
    }
    maps = []
    for core in range(NCORES):
        b, half = core // 2, core % 2
        xe = np.zeros((XROWS, D), f32)
        if half == 0:
            xe[0:HALO] = meta[NMETA - HALO:NMETA]
            xe[HALO:] = x[b, 0:TOK + HALO]
        else:
            xe[0:HALO + TOK] = x[b, TOK - HALO:SEQ]
        invc = np.zeros((128, 32), f32)
        for g, w in enumerate(WINS):
            left = w // 2
            right = w - 1 - left
            for jj in range(8):
                if half == 0:
                    cnt = w
                else:
                    t = T - 8 + jj
                    cnt = min(t + right + 1, T) - max(t - left, 0)
                invc[:, g * 8 + jj] = 1.0 / cnt
        m = dict(shared)
        m["xe"] = xe
        m["invc"] = invc
        maps.append(m)
    return maps


_NC_CACHE = {}


def kernel(**inputs):
    maps = _host_inputs(inputs)
    if "nc" not in _NC_CACHE:
        _NC_CACHE["nc"] = build_nc()
    nc = _NC_CACHE["nc"]
    res = run_bass_kernel_spmd(nc, maps, core_ids=list(range(NCORES)))
    out = np.empty((BATCH, SEQ, D), np.float32)
    for core in range(NCORES):
        b, half = core // 2, core % 2
        out[b, half * TOK:(half + 1) * TOK] = res.results[core]["out"]
    return out
```

```python
import numpy as np
import ml_dtypes
import concourse.bass as bass
import concourse.mybir as mybir
from concourse.bass_utils import run_bass_kernel_spmd

F32 = mybir.dt.float32
BF16 = mybir.dt.bfloat16
ALU = mybir.AluOpType
AF = mybir.ActivationFunctionType

D = 1024
KC = 8
SEQ = 4096
BATCH = 4
NMETA = 16
DFF = 2816
NFC = DFF // 128
NCORES = 8
TOK = 2048
TT = 512
NT = TOK // TT
HALO = 15
EXT = TT + 2 * HALO
EXTP = 544
XROWS = TOK + 2 * HALO
KW = 31
WINS = (2, 4, 8, 16)
RMS_EPS = 1e-6
LN_EPS = 1e-5
NSLOT = 6
RW = 544
CHUNK_EL = 4096
GRAN = 64
PSUM_G0 = 10_000_000
DRAM_G0 = 20_000_000

PC_BGA, PC_BGB, PC_BDW, PC_LNG, PC_LNB, PC_PSC, PC_WDW = 0, 8, 16, 24, 32, 40, 48
PC_COLS = 48 + KC * 4 * 8


class Prod:
    def __init__(self, sem, inc):
        self.sem = sem
        self.inc = inc
        self.count = 0


class Rec(Prod):
    def __init__(self, name, sem, is_pe=False):
        super().__init__(sem, 1)
        self.name = name
        self.is_pe = is_pe
        self.ops = []
        self.waited = {}


class Sched:
    def __init__(self, nc):
        self.nc = nc
        self.bases = {}
        self.lw = {}
        self.rd = {}
        self.gcache = {}

    def gran(self, ap):
        name = ap.tensor.name
        if name not in self.bases:
            return ()
        key = (name, ap.offset, tuple(ap.ap), str(ap.dtype))
        g = self.gcache.get(key)
        if g is not None:
            return g
        base = self.bases[name]
        esz = mybir.dt.size(ap.dtype)
        dims = [(int(s), int(c)) for (s, c) in list(ap.ap)[1:]]
        if not dims:
            dims = [(1, 1)]
        ls, lc = dims[-1]
        outer = dims[:-1]
        span = ((lc - 1) * abs(ls) + 1) * esz if ls != 0 else esz
        offs = [int(ap.offset) * esz]
        for (s, c) in outer:
            if s == 0 or c == 1:
                continue
            offs = [o + i * s * esz for o in offs for i in range(c)]
        gs = set()
        for o in offs:
            lo = (base + o)
            hi = lo + span
            gs.update(range(lo // GRAN, (hi - 1) // GRAN + 1))
        g = frozenset(gs)
        self.gcache[key] = g
        return g

    def _deps(self, rg, wg, prod, val):
        deps = {}

        def upd(p, v):
            if deps.get(p, 0) < v:
                deps[p] = v
        lw, rd = self.lw, self.rd
        for g in rg:
            w = lw.get(g)
            if w is not None:
                upd(*w)
        for g in wg:
            w = lw.get(g)
            if w is not None:
                upd(*w)
            r = rd.get(g)
            if r:
                for p, v in r.items():
                    upd(p, v)
        for g in rg:
            r = rd.get(g)
            if r is None:
                rd[g] = {prod: val}
            else:
                r[prod] = val
        for g in wg:
            lw[g] = (prod, val)
            rd[g] = None
        return deps

    def _waits(self, rec, deps):
        for p, v in deps.items():
            if p is rec:
                if rec.is_pe:
                    continue
                assert v <= rec.count, "self-dependency on unsignaled op"
            if rec.waited.get(p, 0) >= v:
                continue
            rec.waited[p] = v
            rec.ops.append(("wait", p.sem, v))

    def _sets(self, reads, writes, xr, xw):
        rg = set(xr)
        for a in reads:
            rg.update(self.gran(a))
        wg = set(xw)
        for a in writes:
            wg.update(self.gran(a))
        return rg, wg

    def op(self, rec, fn, reads, writes, signal=True, xr=(), xw=()):
        rg, wg = self._sets(reads, writes, xr, xw)
        deps = self._deps(rg, wg, rec, rec.count + 1)
        self._waits(rec, deps)
        rec.ops.append(("ins", fn, signal))
        if signal:
            rec.count += 1

    def dma(self, rec, dsem, out_ap, in_ap, xr=(), xw=(), val=None):
        rg, wg = self._sets([in_ap], [out_ap], xr, xw)
        if val is None:
            val = dsem.count + 16
        deps = self._deps(rg, wg, dsem, val)
        self._waits(rec, deps)
        rec.ops.append(("dma", out_ap, in_ap, dsem.sem))
        dsem.count += 16

    def wait_prod(self, rec, prod):
        if prod.count > 0 and rec.waited.get(prod, 0) < prod.count:
            rec.waited[prod] = prod.count
            rec.ops.append(("wait", prod.sem, prod.count))


def replay(e, rec):
    for o in rec.ops:
        if o[0] == "wait":
            e.wait_ge(o[1], o[2])
        elif o[0] == "ins":
            r = o[1](e)
            if o[2]:
                r.then_inc(rec.sem, 1)
        else:
            e.dma_start(out=o[1], in_=o[2]).then_inc(o[3], 16)


def build_nc(NT=NT):
    TOK = NT * TT
    XROWS = TOK + 2 * HALO
    nc = bass.Bass("TRN2", target_bir_lowering=False)
    S = Sched(nc)

    xe = nc.dram_tensor("xe", [XROWS, D], F32, kind="ExternalInput")
    w_in = nc.dram_tensor("w_in", [D, 5 * D], F32, kind="ExternalInput")
    w_co = nc.dram_tensor("w_conv_out", [D, D], F32, kind="ExternalInput")
    w_pl = nc.dram_tensor("w_pool", [4, 256, 256], F32, kind="ExternalInput")
    w_po = nc.dram_tensor("w_pool_out", [D, D], F32, kind="ExternalInput")
    w_o = nc.dram_tensor("w_o", [D, D], F32, kind="ExternalInput")
    w_fg = nc.dram_tensor("w_ffn_gate", [D, DFF], F32, kind="ExternalInput")
    w_fu = nc.dram_tensor("w_ffn_up", [D, DFF], F32, kind="ExternalInput")
    w_fd = nc.dram_tensor("w_ffn_down", [DFF, D], F32, kind="ExternalInput")
    pc_d = nc.dram_tensor("pc", [128, PC_COLS], F32, kind="ExternalInput")
    gb_d = nc.dram_tensor("gb", [128, 3 * D], F32, kind="ExternalInput")
    id_d = nc.dram_tensor("ident", [128, 128], BF16, kind="ExternalInput")
    ic_d = nc.dram_tensor("invc", [128, 32], F32, kind="ExternalInput")
    em_d = nc.dram_tensor("emask", [128, 32], BF16, kind="ExternalInput")
    out_d = nc.dram_tensor("out", [TOK, D], F32, kind="ExternalOutput")

    chunks = []

    def rect(w, kc0, nkc, c0, ncols):
        src = w.ap().rearrange("(kc p) c -> p kc c", p=128)[:, kc0:kc0 + nkc, c0:c0 + ncols]

        def dst(sc, nkc=nkc, ncols=ncols):
            return sc[:, 0:nkc * ncols].rearrange("p (k c) -> p k c", k=nkc)
        return (dst, src), nkc * ncols

    def add(name, w, kc0, nkc, c0, ncols):
        pr, nel = rect(w, kc0, nkc, c0, ncols)
        chunks.append((name, [pr], nel))

    add("val0", w_in, 0, 8, 0 * D, 512)
    add("gate0", w_in, 0, 8, 1 * D, 512)
    add("val1", w_in, 0, 8, 0 * D + 512, 512)
    add("gate1", w_in, 0, 8, 1 * D + 512, 512)
    chunks.append(("cl0", None, CHUNK_EL))
    add("pool0", w_in, 0, 8, 2 * D, 512)
    chunks.append(("cl1", None, CHUNK_EL))
    add("pool1", w_in, 0, 8, 2 * D + 512, 512)
    prs = []
    for g in range(4):
        src = w_pl.ap()[g].rearrange("(k p) d -> p k d", p=128)

        def dst(sc, g=g):
            return sc[:, g * 512:(g + 1) * 512].rearrange("p (k d) -> p k d", k=2)
        prs.append((dst, src))
    chunks.append(("wpool", prs, 2048))
    for h in range(2):
        add(f"gA{h}", w_in, 0, 8, 3 * D + h * 512, 512)
        add(f"gB{h}", w_in, 0, 8, 4 * D + h * 512, 512)
        add(f"po{h}", w_po, 0, 8, h * 512, 512)
        add(f"co{h}", w_co, 0, 8, h * 512, 512)
    for h in range(2):
        add(f"wo{h}", w_o, 0, 8, h * 512, 512)
    FB = [(b * 512, 512) for b in range(5)] + [(2560, 256)]
    for b, (f0, fw) in enumerate(FB):
        add(f"fg{b}", w_fg, 0, 8, f0, fw)
        add(f"fu{b}", w_fu, 0, 8, f0, fw)
    DG = [(0, 8), (8, 8), (16, 6)]
    for h in range(2):
        for gi, (j0, nj) in enumerate(DG):
            add(f"fd{h}_{gi}", w_fd, j0, nj, h * 512, 512)
    NCH = len(chunks)
    cidx = {c[0]: i for i, c in enumerate(chunks)}
    wsc = nc.dram_tensor("wsc", [NCH, 128, CHUNK_EL], BF16, kind="Internal")

    base0 = (nc.sbuf_base + 63) // 64 * 64
    top = nc.sbuf_top
    cur = [base0]

    def alloc_at(name, shape, dt, off):
        t = nc.alloc_sbuf_tensor_at(name, list(shape), dt, offset=off)
        S.bases[t.name] = off
        return t

    def alloc(name, shape, dt):
        sz = int(np.prod(shape[1:])) * mybir.dt.size(dt)
        off = cur[0]
        cur[0] += (sz + 63) // 64 * 64
        return alloc_at(name, shape, dt, off)

    wslot = [alloc(f"wslot{i}", [128, CHUNK_EL], BF16) for i in range(NSLOT)]
    uT = [alloc(f"uT{i}", [128, KC, EXTP], BF16) for i in range(2)]
    gb = alloc("gb_s", [128, 3, D], F32)
    pc = alloc("pc_s", [128, PC_COLS], F32)
    ident = alloc("ident_s", [128, 128], BF16)
    ones = alloc("ones_s", [128, 128], F32)
    nhalf = alloc("nhalf_s", [128, 16], F32)
    invc = alloc("invc_s", [128, 32], F32)
    emask = alloc("emask_s", [128, 32], BF16)
    repS = [alloc(f"rep{i}", [128, 4, 2, RW], BF16) for i in range(2)]
    sqj = alloc("sqj", [128, D], BF16)
    ub = [alloc(f"ub{i}", [128, D], BF16) for i in range(2)]
    NST = 16
    stats = alloc("stats", [128, NST, 16], F32)
    acarry = alloc("acarry", [128, KC, 32], BF16)
    pcarry = alloc("pcarry", [128, KC, 32], F32)
    tmp8 = alloc("tmp8", [128, 16], F32)

    regA = cur[0]
    aT = alloc("aT", [128, KC, EXTP], BF16)
    sg = [alloc(f"sg{i}", [128, EXT], F32) for i in range(2)]
    psb = [alloc(f"psb{i}", [128, EXT], F32) for i in range(2)]
    s1 = alloc("s1", [128, EXT], F32)
    s2 = alloc("s2", [128, EXT], F32)
    mT = alloc("mT", [128, KC, TT], BF16)
    d_sb = alloc("d_sb", [128, KC, TT], F32)
    dsq = [alloc(f"dsq{i}", [128, TT], F32) for i in range(2)]
    tn = [alloc(f"tn{i}", [128, TT], F32) for i in range(2)]
    tn2 = [alloc(f"tn2{i}", [128, TT], F32) for i in range(2)]
    dst_off = cur[0]
    actT = alloc("actT", [128, KC, TT], BF16)
    m2T = alloc("m2T", [128, KC, TT], BF16)
    lstage = alloc_at("lstage", [128, 128, 32], BF16, dst_off)
    mean = alloc("mean", [128, TT], F32)
    msq = alloc("msq", [128, TT], F32)
    rstd_ln = alloc("rstd_ln", [128, TT], F32)
    mergedT = alloc("mergedT", [128, KC, TT], BF16)
    sgA = [alloc(f"sgA{i}", [128, TT], F32) for i in range(2)]
    sgB = [alloc(f"sgB{i}", [128, TT], F32) for i in range(2)]
    t1 = [alloc("t1_0", [128, TT], F32)] * 2
    t2 = [alloc(f"t2_{i}", [128, TT], F32) for i in range(2)]
    endA = cur[0]
    assert endA <= top, (endA, top)
    cur[0] = regA
    h1 = [alloc(f"h1_{i}", [128, D], F32) for i in range(4)]
    xres = [alloc(f"xres{i}", [128, D], F32) for i in range(4)]
    xt = [xres[0], xres[1]]
    vT = alloc("vT", [128, KC, TT], BF16)
    fT = alloc("fT", [128, NFC, TT], BF16)
    sgf = [alloc(f"sgf{i}", [128, TT], F32) for i in range(2)]
    outb = [alloc(f"outb{i}", [128, D], F32) for i in range(2)]
    assert cur[0] <= endA, (cur[0], endA)

    psum = []
    for i in range(8):
        t = nc.alloc_psum_tensor(f"ps{i}", [128, 512], F32)
        S.bases[t.name] = PSUM_G0 * GRAN + i * 2048
        psum.append(t)
    psum_bf = [p.bitcast(BF16) for p in psum]
    ring = [0]

    def nb():
        b = ring[0]
        ring[0] = (b + 1) % 6
        return b

    def sem(name):
        return nc.alloc_semaphore(name)

    PE = Rec("pe", sem("s_pe"), is_pe=True)
    ACT = Rec("act", sem("s_act"))
    DVE = Rec("dve", sem("s_dve"))
    POOL = Rec("pool", sem("s_pool"))
    SP = Rec("sp", sem("s_sp"))
    cst = Prod(sem("d_const"), 16)
    slot_sem = [Prod(sem(f"d_slot{i}"), 16) for i in range(NSLOT)]
    slot_sem_sw = [Prod(sem(f"d_slotsw{i}"), 16) for i in range(NSLOT)]
    cast_sem = [Prod(sem(f"d_cast{i}"), 16) for i in range(NCH)]
    xt_sem = [Prod(sem(f"d_xt{i}"), 16) for i in range(2)]
    xres_sem = [Prod(sem(f"d_xres{i}"), 16) for i in range(4)]
    out_sem = [Prod(sem(f"d_out{i}"), 16) for i in range(2)]
    rep_sem = [Prod(sem(f"d_rep{i}"), 16) for i in range(2)]

    const_dmas = [
        (gb[:].rearrange("p a d -> p (a d)"), gb_d.ap()),
        (pc[:], pc_d.ap()),
        (ident[:], id_d.ap()),
        (invc[:], ic_d.ap()),
        (emask[:], em_d.ap()),
    ]
    for (o, i_) in const_dmas:
        S.dma(SP, cst, o, i_, val=16 * len(const_dmas))
    S.op(POOL, lambda e: e.memset(ones[:], 1.0), [], [ones[:]])
    S.op(POOL, lambda e: e.memset(nhalf[:], -0.5), [], [nhalf[:]])
    for r_ in repS:
        S.op(POOL, lambda e, r_=r_: e.memset(r_[:], 0.0), [], [r_[:]])

    def pcc(col, c):
        return pc[:, col + c:col + c + 1]

    def cast(n):
        name, prs, nel = chunks[n]
        if prs is None:
            h = int(name[2:])
            do = lstage[:]
            i0 = emask[:].unsqueeze(1).to_broadcast([128, 128, 32])
            wsl = pc[:, PC_WDW + h * 128:PC_WDW + (h + 1) * 128]
            i1 = wsl.unsqueeze(2).to_broadcast([128, 128, 32])
            S.op(POOL, lambda e, do=do, i0=i0, i1=i1: e.tensor_tensor(out=do, in0=i0, in1=i1, op=ALU.mult),
                 [emask[:], wsl], [do])
            S.dma(POOL, cast_sem[n], wsc.ap()[n][:, 0:nel], do.rearrange("p k c -> p (k c)"),
                  xw=[DRAM_G0 + n])
            return
        for pi, (dst, src) in enumerate(prs):
            S.dma(POOL, cast_sem[n], dst(wsc.ap()[n]), src, xw=([DRAM_G0 + n] if pi == 0 else ()),
                  val=16 * len(prs))

    seq = [n for _ in range(NT) for n in range(NCH)]
    NLOADS = len(seq)
    wstate = {"next": 0, "released": 0, "li": 0, "cast": 0}
    CAST_AHEAD = 8

    def load_next():
        li = wstate["next"]
        assert li < wstate["released"] + NSLOT
        n = seq[li]
        s = li % NSLOT
        nel = chunks[n][2]
        prs = chunks[n][1]
        if li < NCH:
            while wstate["cast"] < min(NCH, li + CAST_AHEAD):
                cast(wstate["cast"])
                wstate["cast"] += 1
        S.dma(SP, slot_sem[s], wslot[s][:, 0:nel], wsc.ap()[n][:, 0:nel], xr=[DRAM_G0 + n])
        wstate["next"] = li + 1

    def prefetch():
        while wstate["next"] < min(NLOADS, wstate["released"] + NSLOT):
            load_next()

    def wchunk(name):
        li = wstate["li"]
        assert chunks[seq[li]][0] == name, (chunks[seq[li]][0], name)
        while wstate["next"] <= li:
            load_next()
        wstate["li"] = li + 1
        return wslot[li % NSLOT]

    def release(k):
        if wstate.get("defer"):
            wstate["pending"] = wstate.get("pending", 0) + k
            return
        wstate["released"] += k
        prefetch()

    def defer_releases(on):
        wstate["defer"] = on
        if not on and wstate.get("pending", 0):
            k = wstate["pending"]
            wstate["pending"] = 0
            release(k)

    stat_i = [0]

    def stat_slot():
        i = stat_i[0]
        stat_i[0] = (i + 1) % NST
        return stats[:, i, :]

    def mm(out, lhsT, rhs, start, stop, signal):
        S.op(PE, lambda e: e.matmul(out, lhsT=lhsT, rhs=rhs, start=start, stop=stop),
             [lhsT, rhs], [out], signal=signal)

    def rms_rstd(src, n, gi_unused=None):
        st = stat_slot()
        ssq, ms, rs = st[0:n, 0:1], st[0:n, 1:2], st[0:n, 2:3]
        S.op(ACT, lambda e: e.activation(out=sqj[0:n, :], in_=src, func=AF.Square, accum_out=ssq),
             [src], [sqj[0:n, :], ssq])
        S.op(DVE, lambda e: e.tensor_scalar(out=ms, in0=ssq, scalar1=1.0 / D, scalar2=RMS_EPS,
                                            op0=ALU.mult, op1=ALU.add), [ssq], [ms])
        S.op(POOL, lambda e: e.tensor_tensor(out=rs, in0=ms, in1=nhalf[0:n, 0:1], op=ALU.pow),
             [ms, nhalf[0:n, 0:1]], [rs])
        return rs

    def norm_chain(src, n, gidx, q):
        rs = rms_rstd(src, n)
        u = ub[q][0:n, :]
        g_ap = gb[0:n, gidx, :]
        S.op(DVE, lambda e: e.scalar_tensor_tensor(out=u, in0=src, scalar=rs, in1=g_ap,
                                                   op0=ALU.mult, op1=ALU.mult), [src, rs, g_ap], [u])

    def norm_transpose(n, dstT, col0, q):
        b = nb()
        pb = psum_bf[b]
        for k in range(KC):
            o = pb[:, k * 128:k * 128 + n]
            i_ = ub[q][0:n, k * 128:(k + 1) * 128]
            idn = ident[0:n, 0:n]
            S.op(PE, lambda e, o=o, i_=i_, idn=idn: e.transpose(out=o, in_=i_, identity=idn),
                 [i_, idn], [o], signal=(k == KC - 1))
        src3 = pb[:].rearrange("p (k c) -> p k c", k=KC)[:, :, 0:n]
        dst3 = dstT[:, :, col0:col0 + n]
        S.op(ACT, lambda e: e.activation(out=dst3, in_=src3, func=AF.Copy), [src3], [dst3])

    def stage0_subtiles(i):
        if i == 0:
            return [(r, min(128, EXT - r)) for r in range(0, EXT, 128)]
        return [(2 * HALO + r, 128) for r in range(0, TT, 128)]

    def s0_chain(i, idx):
        r, n = stage0_subtiles(i)[idx]
        q = idx % 2
        row0 = i * TT + r
        S.dma(ACT, xt_sem[q], xt[q][0:n, :], xe.ap()[row0:row0 + n, :])
        norm_chain(xt[q][0:n, :], n, 0, q)

    def s0_transpose(i, idx):
        r, n = stage0_subtiles(i)[idx]
        norm_transpose(n, uT[i % 2], r, idx % 2)

    def stage0_carry(i):
        if i > 0:
            src = uT[(i - 1) % 2][:, :, TT:EXT]
            dst = uT[i % 2][:, :, 0:2 * HALO]
            S.op(DVE, lambda e: e.tensor_copy(out=dst, in_=src), [src], [dst])

    def tile_body(i):
        u = uT[i % 2]
        last = (i == NT - 1)
        pieces = [(0, TT), (TT, 2 * HALO)] if i == 0 else [(2 * HALO, TT)]

        if i > 0:
            dsta = aT[:, :, 0:2 * HALO]
            S.op(POOL, lambda e: e.tensor_copy(out=dsta, in_=acarry[:, :, 0:2 * HALO]), [acarry[:, :, 0:2 * HALO]], [dsta])
        wts = {}
        pend = []

        def stats_mm(c, j):
            mm(psum[6][:], ones[:], d_sb[:, c, :], c == 0, c == KC - 1, c == KC - 1)
            mm(psum[7][:], ones[:], dsq[j][:], c == 0, c == KC - 1, c == KC - 1)

        def vg(c):
            h, cc = c // 4, c % 4
            if cc == 0:
                wts["v"] = wchunk(f"val{h}")
                wts["g"] = wchunk(f"gate{h}")
            wv, wg = wts["v"], wts["g"]
            for (c0, n) in pieces:
                bv, bg = nb(), nb()
                for k in range(KC):
                    mm(psum[bv][:, 0:n], wv[:, k * 512 + cc * 128:k * 512 + cc * 128 + 128],
                       u[:, k, c0:c0 + n], k == 0, k == KC - 1, k == KC - 1)
                for k in range(KC):
                    mm(psum[bg][:, 0:n], wg[:, k * 512 + cc * 128:k * 512 + cc * 128 + 128],
                       u[:, k, c0:c0 + n], k == 0, k == KC - 1, k == KC - 1)
                j = c % 2
                sgo = sg[j][:, 0:n]
                pgi = psum[bg][:, 0:n]
                pvi = psum[bv][:, 0:n]
                ao = aT[:, c, c0:c0 + n]
                S.op(ACT, lambda e, sgo=sgo, pgi=pgi: e.activation(out=sgo, in_=pgi, func=AF.Sigmoid),
                     [pgi], [sgo])
                S.op(DVE, lambda e, ao=ao, pvi=pvi, sgo=sgo: e.tensor_tensor(out=ao, in0=pvi, in1=sgo, op=ALU.mult),
                     [pvi, sgo], [ao])
            if cc == 3:
                release(2)

        def im2col(pr):
            st = pr % 2
            tot = rep_sem[st].count + 16 * 16
            for j in range(4):
                for s_ in range(4):
                    ws = 540 if s_ < 3 else 539
                    o = repS[st][32 * s_:32 * s_ + 32, j, :, 0:ws]
                    i_ = aT[32 * j:32 * j + 32, 2 * pr:2 * pr + 2, s_:s_ + ws]
                    S.dma(SP, rep_sem[st], o, i_, val=tot)

        def conv(c):
            h, cc = c // 4, c % 4
            if cc == 0:
                wts["l"] = wchunk(f"cl{h}")
            wl = wts["l"]
            st, e_ = (c // 2) % 2, c % 2
            bc = nb()
            for q in range(8):
                for j in range(4):
                    col = ((cc * 4 + j) * 8 + q) * 32
                    o = psum[bc][32 * j:32 * j + 32, :]
                    lt = wl[:, col:col + 32]
                    r_ = repS[st][:, j, e_, 4 * q:4 * q + TT]
                    lastmm = (q == 7 and j == 3)
                    S.op(PE, lambda e, o=o, lt=lt, r_=r_, q=q, j=j: e.matmul(
                        o, lhsT=lt, rhs=r_, start=(q == 0), stop=(q == 7), tile_position=(0, 32 * j)),
                        [lt, r_], [o], signal=lastmm)
            if c == 3:
                release(1)
            dso = d_sb[:, c, :]
            pci = psum[bc][:]
            bia = pcc(PC_BDW, c)
            S.op(ACT, lambda e, dso=dso, pci=pci, bia=bia: e.activation(out=dso, in_=pci, func=AF.Identity, bias=bia, scale=1.0),
                 [pci, bia], [dso])
            j2 = c % 2
            dqo = dsq[j2][:]
            S.op(ACT, lambda e, dqo=dqo, dso=dso: e.activation(out=dqo, in_=dso, func=AF.Square), [dso], [dqo])
            pend.append((c, j2))
            if len(pend) > 1:
                stats_mm(*pend.pop(0))

        def poolcols(h):
            wp = wchunk(f"pool{h}")
            PENG = POOL if h == 0 else DVE
            for cc in range(4):
                c = h * 4 + cc
                j = c % 2
                p = psb[j]
                if i > 0:
                    dstp = p[:, 0:2 * HALO]
                    srcp = pcarry[:, c, 0:2 * HALO]
                    S.op(PENG, lambda e, dstp=dstp, srcp=srcp: e.tensor_copy(out=dstp, in_=srcp), [srcp], [dstp])
                for (c0, n) in pieces:
                    bp = nb()
                    for k in range(KC):
                        mm(psum[bp][:, 0:n], wp[:, k * 512 + cc * 128:k * 512 + cc * 128 + 128],
                           u[:, k, c0:c0 + n], k == 0, k == KC - 1, k == KC - 1)
                    po = p[:, c0:c0 + n]
                    pi_ = psum[bp][:, 0:n]
                    S.op(ACT, lambda e, po=po, pi_=pi_: e.activation(out=po, in_=pi_, func=AF.Copy), [pi_], [po])
                if not last:
                    srcp = p[:, TT:EXT]
                    dstp = pcarry[:, c, 0:2 * HALO]
                    S.op(PENG, lambda e, dstp=dstp, srcp=srcp: e.tensor_copy(out=dstp, in_=srcp), [srcp], [dstp])
                g = c // 2
                w = WINS[g]
                srcb, width = p, EXT
                step = 1
                tgl = 0
                while step < w:
                    dstb = (s1, s2)[tgl]
                    nw = width - step
                    a0 = srcb[:, 0:nw]
                    a1 = srcb[:, step:step + nw]
                    do = dstb[:, 0:nw]
                    S.op(PENG, lambda e, do=do, a0=a0, a1=a1: e.tensor_tensor(out=do, in0=a0, in1=a1, op=ALU.add),
                         [a0, a1], [do])
                    srcb, width = dstb, nw
                    step *= 2
                    tgl ^= 1
                off = HALO - w // 2
                Sw = srcb[:, off:off + TT]
                pc_ = p[:, HALO:HALO + TT]
                mo = mT[:, c, :]
                S.op(DVE, lambda e, mo=mo, Sw=Sw, pc_=pc_, w=w: e.scalar_tensor_tensor(
                    out=mo, in0=Sw, scalar=1.0 / w, in1=pc_, op0=ALU.mult, op1=ALU.subtract), [Sw, pc_], [mo])
                if last:
                    Sw8 = srcb[:, off + TT - 8:off + TT]
                    ic8 = invc[:, g * 8:(g + 1) * 8]
                    p8 = p[:, HALO + TT - 8:HALO + TT]
                    t8 = tmp8[:, 0:8]
                    mo8 = mT[:, c, TT - 8:TT]
                    S.op(DVE, lambda e, t8=t8, Sw8=Sw8, ic8=ic8: e.tensor_tensor(out=t8, in0=Sw8, in1=ic8, op=ALU.mult),
                         [Sw8, ic8], [t8])
                    S.op(DVE, lambda e, mo8=mo8, t8=t8, p8=p8: e.tensor_tensor(out=mo8, in0=t8, in1=p8, op=ALU.subtract),
                         [t8, p8], [mo8])

        vg(0); vg(1); im2col(0)
        defer_releases(True)
        vg(2); vg(3); im2col(1)
        vg(4); vg(5)
        conv(0); conv(1); im2col(2)
        defer_releases(False)
        vg(6); vg(7)
        conv(2); conv(3); im2col(3)
        poolcols(0)
        release(1)
        conv(4); conv(5)
        poolcols(1)
        conv(6); conv(7)
        release(2)
        if not last:
            srca = aT[:, :, TT:EXT]
            S.op(POOL, lambda e, srca=srca: e.tensor_copy(out=acarry[:, :, 0:2 * HALO], in_=srca), [srca], [acarry[:, :, 0:2 * HALO]])
        stats_mm(*pend.pop(0))
        wpl = wchunk("wpool")

        def wpool_group(gi):
            g, jo = gi // 2, gi % 2
            c = 2 * g + jo
            bw = nb()
            for k in range(2):
                mm(psum[bw][:], wpl[:, g * 512 + k * 256 + jo * 128:g * 512 + k * 256 + jo * 128 + 128],
                   mT[:, 2 * g + k, :], k == 0, k == 1, k == 1)
            m2o = m2T[:, c, :]
            pwi = psum[bw][:]
            sc = pcc(PC_PSC, c)
            S.op(ACT, lambda e, m2o=m2o, pwi=pwi, sc=sc: e.activation(out=m2o, in_=pwi, func=AF.Identity, scale=sc),
                 [pwi, sc], [m2o])
            if gi == 7:
                release(1)

        def ln_stats():
            S.op(DVE, lambda e: e.tensor_scalar(out=mean[:], in0=psum[6][:], scalar1=1.0 / D, scalar2=None, op0=ALU.mult),
                 [psum[6][:]], [mean[:]])
            S.op(DVE, lambda e: e.tensor_tensor(out=msq[:], in0=mean[:], in1=mean[:], op=ALU.mult), [mean[:]], [msq[:]])
            S.op(DVE, lambda e: e.scalar_tensor_tensor(out=msq[:], in0=psum[7][:], scalar=1.0 / D, in1=msq[:],
                                                       op0=ALU.mult, op1=ALU.subtract), [psum[7][:], msq[:]], [msq[:]])
            S.op(DVE, lambda e: e.tensor_scalar(out=msq[:], in0=msq[:], scalar1=LN_EPS, scalar2=None, op0=ALU.add),
                 [msq[:]], [msq[:]])

        def ln_sqrt():
            S.op(ACT, lambda e: e.activation(out=rstd_ln[:], in_=msq[:], func=AF.Sqrt), [msq[:]], [rstd_ln[:]])

        def ln_recip():
            S.op(DVE, lambda e: e.reciprocal(out=rstd_ln[:], in_=rstd_ln[:]), [rstd_ln[:]], [rstd_ln[:]])

        def ln_norm(c):
            j = c % 2
            dso = d_sb[:, c, :]
            tno = tn[j][:]
            tn2o = tn2[j][:]
            ao = actT[:, c, :]
            lg, lb = pcc(PC_LNG, c), pcc(PC_LNB, c)
            S.op(DVE, lambda e, tno=tno, dso=dso: e.tensor_tensor(out=tno, in0=dso, in1=mean[:], op=ALU.subtract),
                 [dso, mean[:]], [tno])
            S.op(DVE, lambda e, tn2o=tn2o, tno=tno: e.tensor_tensor(out=tn2o, in0=tno, in1=rstd_ln[:], op=ALU.mult),
                 [tno, rstd_ln[:]], [tn2o])
            S.op(ACT, lambda e, ao=ao, tn2o=tn2o, lg=lg, lb=lb: e.activation(out=ao, in_=tn2o, func=AF.Silu, bias=lb, scale=lg),
                 [tn2o, lg, lb], [ao])

        mw = {}

        def wsl(wt, k, cc):
            return wt[:, k * 512 + cc * 128:k * 512 + cc * 128 + 128]

        gbanks = {}

        def GA(c):
            h, cc = c // 4, c % 4
            j = c % 2
            if cc == 0:
                mw["ga"] = wchunk(f"gA{h}")
                mw["gb"] = wchunk(f"gB{h}")
            bA, bB = nb(), nb()
            for k in range(KC):
                mm(psum[bA][:], wsl(mw["ga"], k, cc), u[:, k, HALO:HALO + TT], k == 0, k == KC - 1, k == KC - 1)
            for k in range(KC):
                mm(psum[bB][:], wsl(mw["gb"], k, cc), u[:, k, HALO:HALO + TT], k == 0, k == KC - 1, k == KC - 1)
            sa, sb_ = sgA[j][:], sgB[j][:]
            pa, pb_ = psum[bA][:], psum[bB][:]
            ba, bb = pcc(PC_BGA, c), pcc(PC_BGB, c)
            S.op(ACT, lambda e, sa=sa, pa=pa, ba=ba: e.activation(out=sa, in_=pa, func=AF.Sigmoid, bias=ba, scale=1.0),
                 [pa, ba], [sa])
            S.op(ACT, lambda e, sb_=sb_, pb_=pb_, bb=bb: e.activation(out=sb_, in_=pb_, func=AF.Sigmoid, bias=bb, scale=1.0),
                 [pb_, bb], [sb_])

        def GP(c):
            h, cc = c // 4, c % 4
            j = c % 2
            if cc == 0:
                mw["po"] = wchunk(f"po{h}")
            bP = nb()
            for k in range(KC):
                mm(psum[bP][:], wsl(mw["po"], k, cc), m2T[:, k, :], k == 0, k == KC - 1, k == KC - 1)
            if cc == 3:
                release(3)
            sb_ = sgB[j][:]
            pp = psum[bP][:]
            t2o = t2[j][:]
            S.op(DVE, lambda e, t2o=t2o, pp=pp, sb_=sb_: e.tensor_tensor(out=t2o, in0=pp, in1=sb_, op=ALU.mult),
                 [pp, sb_], [t2o])

        def Y(c):
            h, cc = c // 4, c % 4
            j = c % 2
            if cc == 0:
                mw["co"] = wchunk(f"co{h}")
            bC = nb()
            for k in range(KC):
                mm(psum[bC][:], wsl(mw["co"], k, cc), actT[:, k, :], k == 0, k == KC - 1, k == KC - 1)
            if cc == 3:
                release(1)
            sa = sgA[j][:]
            pcv = psum[bC][:]
            t1o, t2o = t1[0][:], t2[j][:]
            S.op(DVE, lambda e, t1o=t1o, pcv=pcv, sa=sa: e.tensor_tensor(out=t1o, in0=pcv, in1=sa, op=ALU.mult),
                 [pcv, sa], [t1o])
            mo = mergedT[:, c, :]
            S.op(POOL, lambda e, mo=mo, t1o=t1o, t2o=t2o: e.tensor_tensor(out=mo, in0=t1o, in1=t2o, op=ALU.add),
                 [t1o, t2o], [mo])

        ln_stats()
        GA(0)
        ln_sqrt()
        ln_recip()
        GA(1)
        for gi in range(5):
            wpool_group(gi)
        ln_norm(0)
        for gi in range(5, 8):
            wpool_group(gi)
        ln_norm(1)
        ln_norm(2)
        GP(0)
        ln_norm(3)
        GP(1)
        for c in range(4, KC):
            ln_norm(c)
        for c in range(KC):
            Y(c)
            if c + 2 < KC:
                GA(c + 2)
                GP(c + 2)

        for s in range(4):
            row0 = i * TT + HALO + s * 128
            S.dma(ACT, xres_sem[s], xres[s][:], xe.ap()[row0:row0 + 128, :])
        wos = [wchunk("wo0"), wchunk("wo1")]
        for s_ in range(4):
            for h in range(2):
                bo = nb()
                for k in range(KC):
                    mm(psum[bo][:], mergedT[:, k, s_ * 128:(s_ + 1) * 128], wos[h][:, k * 512:(k + 1) * 512],
                       k == 0, k == KC - 1, k == KC - 1)
                ho = h1[s_][:, h * 512:(h + 1) * 512]
                xi = xres[s_][:, h * 512:(h + 1) * 512]
                poi = psum[bo][:]
                S.op(DVE, lambda e, ho=ho, poi=poi, xi=xi: e.tensor_tensor(out=ho, in0=poi, in1=xi, op=ALU.add),
                     [poi, xi], [ho])
            norm_chain(h1[s_][:], 128, 1, s_ % 2)
            if s_ >= 1:
                norm_transpose(128, vT, (s_ - 1) * 128, (s_ - 1) % 2)
        release(2)
        nxt = stage0_subtiles(i + 1) if i + 1 < NT else []
        ffw = {0: (wchunk("fg0"), wchunk("fu0"))}
        NSPL = 2
        pre = {}
        H2 = TT // 2
        for jj in range(NSPL):
            bG, bU = nb(), nb()
            pre[jj] = (bG, bU)
            for wt_, bk in ((ffw[0][0], bG), (ffw[0][1], bU)):
                for k in range(KC):
                    mm(psum[bk][:, 0:H2], wt_[:, k * 512 + jj * 128:k * 512 + jj * 128 + 128], vT[:, k, 0:H2],
                       k == 0, k == KC - 1, k == KC - 1)
        norm_transpose(128, vT, 3 * 128, 1)
        if nxt:
            stage0_carry(i + 1)
            s0_chain(i + 1, 0)

        for b, (f0, fw) in enumerate(FB):
            if b == 0:
                wg_, wu_ = ffw[0]
            else:
                wg_ = wchunk(f"fg{b}")
                wu_ = wchunk(f"fu{b}")
            for jj in range(fw // 128):
                jf = f0 // 128 + jj
                if b == 0 and jj < NSPL:
                    bG, bU = pre[jj]
                    for wt_, bk in ((wg_, bG), (wu_, bU)):
                        for k in range(KC):
                            mm(psum[bk][:, H2:TT], wt_[:, k * fw + jj * 128:k * fw + jj * 128 + 128], vT[:, k, H2:TT],
                               k == 0, k == KC - 1, k == KC - 1)
                else:
                    bG, bU = nb(), nb()
                    for k in range(KC):
                        mm(psum[bG][:], wg_[:, k * fw + jj * 128:k * fw + jj * 128 + 128], vT[:, k, :],
                           k == 0, k == KC - 1, k == KC - 1)
                    for k in range(KC):
                        mm(psum[bU][:], wu_[:, k * fw + jj * 128:k * fw + jj * 128 + 128], vT[:, k, :],
                           k == 0, k == KC - 1, k == KC - 1)
                jx = jf % 2
                so = sgf[jx][:]
                pgi, pui = psum[bG][:], psum[bU][:]
                fo = fT[:, jf, :]
                S.op(ACT, lambda e, so=so, pgi=pgi: e.activation(out=so, in_=pgi, func=AF.Silu), [pgi], [so])
                S.op(DVE, lambda e, fo=fo, pui=pui, so=so: e.tensor_tensor(out=fo, in0=pui, in1=so, op=ALU.mult),
                     [pui, so], [fo])
            release(2)
            if b + 1 < len(nxt):
                s0_chain(i + 1, b + 1)
            if b < len(nxt):
                s0_transpose(i + 1, b)
        oq = [0]
        for h in range(2):
            wd = [wchunk(f"fd{h}_{gi}") for gi in range(3)]
            for s in range(4):
                bd = nb()
                for jf in range(NFC):
                    mm(psum[bd][:], fT[:, jf, s * 128:(s + 1) * 128],
                       wd[jf // 8][:, (jf % 8) * 512:(jf % 8) * 512 + 512], jf == 0, jf == NFC - 1, jf == NFC - 1)
                ho = h1[s][:, h * 512:(h + 1) * 512]
                pdi = psum[bd][:]
                S.op(DVE, lambda e, ho=ho, pdi=pdi: e.tensor_tensor(out=ho, in0=pdi, in1=ho, op=ALU.add),
                     [pdi, ho], [ho])
                if h == 1:
                    rs = rms_rstd(h1[s][:], 128)
                    q = oq[0]
                    oq[0] ^= 1
                    oo = outb[q][:]
                    hs = h1[s][:]
                    gf = gb[:, 2, :]
                    S.op(DVE, lambda e, oo=oo, hs=hs, rs=rs, gf=gf: e.scalar_tensor_tensor(
                        out=oo, in0=hs, scalar=rs, in1=gf, op0=ALU.mult, op1=ALU.mult), [hs, rs, gf], [oo])
                    r0 = i * TT + s * 128
                    S.dma(POOL, out_sem[q], out_d.ap()[r0:r0 + 128, :], oo)
            release(3)

    for n in range(6):
        cast(n)
    wstate["cast"] = 6
    nsub0 = len(stage0_subtiles(0))
    s0_chain(0, 0)
    for idx in range(nsub0):
        if idx + 1 < nsub0:
            s0_chain(0, idx + 1)
        s0_transpose(0, idx)
    prefetch()
    last_dg = max(n for n in range(NCH) if chunks[n][1] is None)
    while wstate["cast"] <= last_dg:
        cast(wstate["cast"])
        wstate["cast"] += 1
    for i in range(NT):
        tile_body(i)
    for q in range(2):
        S.wait_prod(POOL, out_sem[q])

    with nc.Block() as block:
        @block.tensor
        def _(e):
            replay(e, PE)

        @block.scalar
        def _(e):
            replay(e, ACT)

        @block.vector
        def _(e):
            replay(e, DVE)

        @block.gpsimd
        def _(e):
            replay(e, POOL)

        @block.sync
        def _(e):
            replay(e, SP)
    return nc


def _host_inputs(inputs):
    f32 = np.float32
    x = np.asarray(inputs["x"], f32)
    meta = np.asarray(inputs["meta_tokens"], f32)

    def fm(v):
        return np.ascontiguousarray(np.asarray(v, f32).reshape(KC, 128).T)

    pc = np.zeros((128, PC_COLS), f32)
    bg = np.asarray(inputs["b_gate"], f32)[0]
    pc[:, PC_BGA:PC_BGA + 8] = fm(bg[:D])
    pc[:, PC_BGB:PC_BGB + 8] = fm(bg[D:])
    pc[:, PC_BDW:PC_BDW + 8] = fm(inputs["b_dw"][0])
    pc[:, PC_LNG:PC_LNG + 8] = fm(inputs["ln_g"][0])
    pc[:, PC_LNB:PC_LNB + 8] = fm(inputs["ln_b"][0])
    pc[:, PC_PSC:PC_PSC + 8] = fm(inputs["pool_scale"][0])
    wdw = np.asarray(inputs["w_dw"], f32)[0]
    wpad = np.concatenate([wdw, np.zeros((1, D), f32)], 0)
    w6 = wpad.reshape(8, 4, KC, 4, 32)
    wrep = w6.transpose(1, 4, 2, 3, 0).reshape(128, KC * 4 * 8)
    pc[:, PC_WDW:] = wrep
    emask = np.zeros((128, 32), ml_dtypes.bfloat16)
    for s_ in range(4):
        emask[32 * s_ + np.arange(32), np.arange(32)] = 1
    gb = np.stack([np.asarray(inputs["g_mix"], f32)[0], np.asarray(inputs["g_ffn"], f32)[0],
                   np.asarray(inputs["g_final"], f32)], 0).reshape(1, 3 * D)
    gb = np.ascontiguousarray(np.broadcast_to(gb, (128, 3 * D)))
    ident = np.eye(128, dtype=ml_dtypes.bfloat16)
    T = SEQ + NMETA
    shared = {
        "w_in": np.ascontiguousarray(np.asarray(inputs["w_in"], f32)[0]),
        "w_conv_out": np.ascontiguousarray(np.asarray(inputs["w_conv_out"], f32)[0]),
        "w_pool": np.ascontiguousarray(np.asarray(inputs["w_pool"], f32)[0]),
        "w_pool_out": np.ascontiguousarray(np.asarray(inputs["w_pool_out"], f32)[0]),
        "w_o": np.ascontiguousarray(np.asarray(inputs["w_o"], f32)[0]),
        "w_ffn_gate": np.ascontiguousarray(np.asarray(inputs["w_ffn_gate"], f32)[0]),
        "w_ffn_up": np.ascontiguousarray(np.asarray(inputs["w_ffn_up"], f32)[0]),
        "w_ffn_down": np.ascontiguousarray(np.asarray(inputs["w_ffn_down"], f32)[0]),
        "pc": pc, "gb": gb, "ident": ident, "emask": emask,
    }
    maps = []
    for core in range(NCORES):
        b, half = core // 2, core % 2
        xe = np.zeros((XROWS, D), f32)
        if half == 0:
            xe[0:HALO] = meta[NMETA - HALO:NMETA]
            xe[HALO:] = x[b, 0:TOK + HALO]
        else:
            xe[0:HALO + TOK] = x[b, TOK - HALO:SEQ]
        invc = np.zeros((128, 32), f32)
        for g, w in enumerate(WINS):
            left = w // 2
            right = w - 1 - left
            for jj in range(8):
                if half == 0:
                    cnt = w
                else:
                    t = T - 8 + jj
                    cnt = min(t + right + 1, T) - max(t - left, 0)
                invc[:, g * 8 + jj] = 1.0 / cnt
        m = dict(shared)
        m["xe"] = xe
        m["invc"] = invc
        maps.append(m)
    return maps


_NC_CACHE = {}


def kernel(**inputs):
    maps = _host_inputs(inputs)
    if "nc" not in _NC_CACHE:
        _NC_CACHE["nc"] = build_nc()
    nc = _NC_CACHE["nc"]
    res = run_bass_kernel_spmd(nc, maps, core_ids=list(range(NCORES)))
    out = np.empty((BATCH, SEQ, D), np.float32)
    for core in range(NCORES):
        b, half = core // 2, core % 2
        out[b, half * TOK:(half + 1) * TOK] = res.results[core]["out"]
    return out
```
